# Optimizing a Trainium2 kernel written in Bass

```python
import math
import jax, jax.numpy as jnp
from jax import lax
import numpy as np

D_MODEL = 2048
BATCH = 1
SEQ = 8192
DEPTH = 4

CHUNK = 64
N_MIXERS = 3
N_A = (DEPTH + 2) // 3
N_B = (DEPTH + 1) // 3
N_C = DEPTH // 3
CONV_WIDTH = 31
POOL_WINDOWS = (2, 4, 8, 16)
N_POOL_GROUPS = len(POOL_WINDOWS)
POOL_GC = D_MODEL // N_POOL_GROUPS
HEAD_DIM = 64
N_HEADS = D_MODEL // HEAD_DIM
N_KV = 8
GROUP = N_HEADS // N_KV
WINDOW = 128
WINDOW_CHUNKS = WINDOW // CHUNK
QBLOCK = 128
NUM_BUCKETS = 32
REL_MAX_DIST = 128
D_FF = ((8 * D_MODEL // 3 + 255) // 256) * 256
PLE_DIM = 256
EPS = 1e-6
NEG_INF = -1e30

kernel_name = "hybrid_conv_pool_swa_trunk"


def rms_norm(x, g):
    xf = x.astype(jnp.float32)
    y = xf * lax.rsqrt(jnp.mean(xf * xf, axis=-1, keepdims=True) + EPS)
    return (y * g.astype(jnp.float32)).astype(x.dtype)


def layer_norm(x, g, b):
    xf = x.astype(jnp.float32)
    mu = jnp.mean(xf, axis=-1, keepdims=True)
    xc = xf - mu
    y = xc * lax.rsqrt(jnp.mean(xc * xc, axis=-1, keepdims=True) + EPS)
    return (y * g.astype(jnp.float32) + b.astype(jnp.float32)).astype(x.dtype)


def conformer_conv(h, w_in, b_in, w_dw, b_dw, ln_g, ln_b, w_out, b_out):
    u = h @ w_in + b_in
    a, gate = jnp.split(u, 2, axis=-1)
    u = a * jax.nn.sigmoid(gate)
    u = lax.conv_general_dilated(
        u, w_dw[:, None, :], window_strides=(1,), padding=[(CONV_WIDTH - 1, 0)],
        dimension_numbers=("NWC", "WIO", "NWC"), feature_group_count=D_MODEL) + b_dw
    u = jax.nn.silu(layer_norm(u, ln_g, ln_b))
    return u @ w_out + b_out


def multiscale_pool(h, w_grp, scale):
    B, S, D = h.shape
    hf = h.astype(jnp.float32)
    cs = jnp.concatenate([jnp.zeros((B, 1, D), jnp.float32), jnp.cumsum(hf, axis=1)], axis=1)
    t = jnp.arange(S)
    pooled = []
    for g, w in enumerate(POOL_WINDOWS):
        sl = slice(g * POOL_GC, (g + 1) * POOL_GC)
        start = jnp.maximum(t + 1 - w, 0)
        s = cs[:, 1:, sl] - cs[:, start, sl]
        cnt = (t + 1 - start).astype(jnp.float32)[None, :, None]
        pooled.append(s / cnt)
    pooled = jnp.stack(pooled, axis=2)
    mix = (pooled - hf.reshape(B, S, N_POOL_GROUPS, POOL_GC)).astype(h.dtype)
    y = jnp.einsum('bsgc,gcd->bsgd', mix, w_grp).reshape(B, S, D)
    return y * scale


def t5_bucket(rel):
    nb = NUM_BUCKETS // 2
    n = -rel
    ret = jnp.where(n < 0, nb, 0)
    n = jnp.abs(n)
    max_exact = nb // 2
    nf = jnp.maximum(n, 1).astype(jnp.float32)
    large = max_exact + (jnp.log(nf / max_exact) / math.log(REL_MAX_DIST / max_exact)
                         * (nb - max_exact)).astype(jnp.int32)
    large = jnp.minimum(large, nb - 1)
    return ret + jnp.where(n < max_exact, n, large)


def band_bias_and_mask(rel_bias, n_blocks):
    i = jnp.arange(QBLOCK)[:, None]
    j = jnp.arange(2 * QBLOCK)[None, :]
    rel = j - QBLOCK - i
    bias = rel_bias[t5_bucket(rel)]
    bias = jnp.transpose(bias, (2, 0, 1)).reshape(N_KV, GROUP, QBLOCK, 2 * QBLOCK)
    qc = i // CHUNK
    kc = jnp.floor_divide(j - QBLOCK, CHUNK)
    chunk_ok = (kc <= qc) & (kc >= qc - WINDOW_CHUNKS)
    blk = jnp.arange(n_blocks)[:, None, None]
    mask = chunk_ok[None] & ((blk > 0) | (j[None] >= QBLOCK))
    return bias, mask


def swa_sink_attention(h, w_qkv, q_g, k_g, sinks, w_o, rel_bias):
    B, S, _ = h.shape
    NB = S // QBLOCK
    qkv = h @ w_qkv
    q, k, v = jnp.split(qkv, [N_HEADS * HEAD_DIM, (N_HEADS + N_KV) * HEAD_DIM], axis=-1)
    q = rms_norm(q.reshape(B, S, N_KV, GROUP, HEAD_DIM), q_g)
    k = rms_norm(k.reshape(B, S, N_KV, HEAD_DIM), k_g)
    v = v.reshape(B, S, N_KV, HEAD_DIM)
    q = q.reshape(B, NB, QBLOCK, N_KV, GROUP, HEAD_DIM)

    def band(t):
        tb = t.reshape(B, NB, QBLOCK, N_KV, HEAD_DIM)
        prev = jnp.concatenate([jnp.zeros_like(tb[:, :1]), tb[:, :-1]], axis=1)
        return jnp.concatenate([prev, tb], axis=2)

    kb, vb = band(k), band(v)
    bias, mask = band_bias_and_mask(rel_bias, NB)
    logits = jnp.einsum('bnqhgd,bnkhd->bnhgqk', q, kb,
                        preferred_element_type=jnp.float32) * (HEAD_DIM ** -0.5)
    logits = logits + bias.astype(jnp.float32)
    logits = jnp.where(mask[None, :, None, None], logits, NEG_INF)
    sink = sinks.astype(jnp.float32).reshape(1, 1, N_KV, GROUP, 1, 1)
    m = jnp.maximum(jnp.max(logits, axis=-1, keepdims=True), sink)
    e = jnp.exp(logits - m)
    denom = jnp.sum(e, axis=-1, keepdims=True) + jnp.exp(sink - m)
    probs = (e / denom).astype(v.dtype)
    o = jnp.einsum('bnhgqk,bnkhd->bnqhgd', probs, vb).reshape(B, S, N_HEADS * HEAD_DIM)
    return o @ w_o


def setup_inputs(seed: int = 0) -> dict:
    key = jax.random.key(seed)
    ks = jax.random.split(key, 32)
    f32 = jnp.float32
    nrm = lambda k, shape, scale: jax.random.normal(k, shape, f32) * scale
    gain = lambda k, shape: 1.0 + 0.02 * jax.random.normal(k, shape, f32)
    D = D_MODEL
    return {
        "x": nrm(ks[0], (BATCH, SEQ, D), 1.0),
        "p": nrm(ks[1], (DEPTH, BATCH, SEQ, PLE_DIM), 1.0),
        "norm_mix": gain(ks[2], (DEPTH, D)),
        "norm_ffn": gain(ks[3], (DEPTH, D)),
        "norm_ple": gain(ks[4], (DEPTH, D)),
        "conv_w_in": nrm(ks[5], (N_A, D, 2 * D), D ** -0.5),
        "conv_b_in": nrm(ks[6], (N_A, 2 * D), 0.02),
        "conv_w_dw": nrm(ks[7], (N_A, CONV_WIDTH, D), CONV_WIDTH ** -0.5),
        "conv_b_dw": nrm(ks[8], (N_A, D), 0.02),
        "conv_ln_g": gain(ks[9], (N_A, D)),
        "conv_ln_b": nrm(ks[10], (N_A, D), 0.02),
        "conv_w_out": nrm(ks[11], (N_A, D, D), D ** -0.5),
        "conv_b_out": nrm(ks[12], (N_A, D), 0.02),
        "pool_w": nrm(ks[13], (N_B, N_POOL_GROUPS, POOL_GC, POOL_GC), POOL_GC ** -0.5),
        "pool_scale": 0.5 + 0.05 * jax.random.normal(ks[14], (N_B, D), f32),
        "attn_w_qkv": nrm(ks[15], (N_C, D, (N_HEADS + 2 * N_KV) * HEAD_DIM), D ** -0.5),
        "attn_q_norm": gain(ks[16], (N_C, HEAD_DIM)),
        "attn_k_norm": gain(ks[17], (N_C, HEAD_DIM)),
        "attn_sinks": nrm(ks[18], (N_C, N_HEADS), 0.5),
        "attn_w_o": nrm(ks[19], (N_C, N_HEADS * HEAD_DIM, D), (N_HEADS * HEAD_DIM) ** -0.5),
        "rel_bias": nrm(ks[20], (NUM_BUCKETS, N_HEADS), 0.5),
        "ffn_w_gate": nrm(ks[21], (DEPTH, D, D_FF), D ** -0.5),
        "ffn_w_up": nrm(ks[22], (DEPTH, D, D_FF), D ** -0.5),
        "ffn_w_down": nrm(ks[23], (DEPTH, D_FF, D), D_FF ** -0.5),
        "ple_w_proj": nrm(ks[24], (DEPTH, PLE_DIM, D), PLE_DIM ** -0.5),
        "ple_w_gate": nrm(ks[25], (DEPTH, D, D), D ** -0.5),
        "ple_b_gate": nrm(ks[26], (DEPTH, D), 0.02),
    }


def reference(x, p, norm_mix, norm_ffn, norm_ple,
              conv_w_in, conv_b_in, conv_w_dw, conv_b_dw, conv_ln_g, conv_ln_b, conv_w_out, conv_b_out,
              pool_w, pool_scale,
              attn_w_qkv, attn_q_norm, attn_k_norm, attn_sinks, attn_w_o, rel_bias,
              ffn_w_gate, ffn_w_up, ffn_w_down,
              ple_w_proj, ple_w_gate, ple_b_gate):
    for i in range(DEPTH):
        kind, j = i % N_MIXERS, i // N_MIXERS
        h = rms_norm(x, norm_mix[i])
        if kind == 0:
            y = conformer_conv(h, conv_w_in[j], conv_b_in[j], conv_w_dw[j], conv_b_dw[j],
                               conv_ln_g[j], conv_ln_b[j], conv_w_out[j], conv_b_out[j])
        elif kind == 1:
            y = multiscale_pool(h, pool_w[j], pool_scale[j])
        else:
            y = swa_sink_attention(h, attn_w_qkv[j], attn_q_norm[j], attn_k_norm[j],
                                   attn_sinks[j], attn_w_o[j], rel_bias)
        x = x + y
        h = rms_norm(x, norm_ffn[i])
        x = x + (jax.nn.silu(h @ ffn_w_gate[i]) * (h @ ffn_w_up[i])) @ ffn_w_down[i]
        g = jax.nn.sigmoid(rms_norm(x, norm_ple[i]) @ ple_w_gate[i] + ple_b_gate[i])
        x = x + g * (p[i] @ ple_w_proj[i])
    return x
```

```python
import math
from contextlib import ExitStack

import numpy as np
import concourse.bass as bass
import concourse.mybir as mybir
from concourse.bass_utils import run_bass_kernel_spmd

F32 = mybir.dt.float32
BF16 = mybir.dt.bfloat16
AF = mybir.ActivationFunctionType
ALU = mybir.AluOpType

ENGS = ("pe", "act", "dve", "pool", "sp")


class Op:
    __slots__ = ("eng", "fn", "edeps", "dwaits", "dma", "sig", "sigidx", "opid")


class Prog:
    def __init__(self, same_engine_sync=True):
        self.ops = []
        self.cells = {}
        self.gran = {}
        self.rowsz = {}
        self.dma_cnt = {}
        self.cell_cache = {}
        self.same_engine_sync = same_engine_sync

    def track(self, handle, gran):
        row = 1
        for s in list(handle.shape)[1:]:
            row *= s
        self.gran[handle.name] = gran
        self.rowsz[handle.name] = row

    def cells_of(self, ap):
        name = ap.name
        g = self.gran.get(name)
        if g is None:
            return ()
        key = (name, ap.offset, ap.ap)
        r = self.cell_cache.get(key)
        if r is not None:
            return r
        row = self.rowsz[name]
        off = ap.offset
        dims = ap.ap
        pcnt = dims[0][1]
        p0 = off // row
        foff = off % row
        p1 = p0 + pcnt
        halves = []
        if p0 < 64:
            halves.append(0)
        if p1 > 64:
            halves.append(1)
        starts = [foff]
        free = dims[1:]
        for (st, cnt) in free[:-1]:
            starts = [s + i * st for s in starts for i in range(cnt)]
        if free:
            lst, lcnt = free[-1]
            ext = (lcnt - 1) * abs(lst) + 1
        else:
            ext = 1
        cs = set()
        for s in starts:
            for c in range(s // g, (s + ext - 1) // g + 1):
                for h in halves:
                    cs.add((name, h, c))
        r = tuple(cs)
        self.cell_cache[key] = r
        return r

    def add(self, eng, fn, reads=(), writes=(), dma=None):
        op = Op()
        op.eng = eng
        op.fn = fn
        op.dma = dma
        op.sig = False
        op.sigidx = 0
        op.opid = len(self.ops)
        chan = dma if dma is not None else eng
        edeps = set()
        dwaits = {}
        cells = self.cells
        ops = self.ops
        rcells = []
        wcells = []
        for ap in reads:
            rcells.extend(self.cells_of(ap))
        for ap in writes:
            wcells.extend(self.cells_of(ap))

        def dep(d):
            p = ops[d]
            if p.dma is not None:
                v = self.dma_cnt[p.dma] * 16
                if dwaits.get(p.dma, 0) < v:
                    dwaits[p.dma] = v
            else:
                if p.eng == eng and dma is None:
                    if eng == "pe" or not self.same_engine_sync:
                        return
                edeps.add(d)

        for c in rcells:
            st = cells.get(c)
            if st is not None and st[0] >= 0:
                dep(st[0])
        for c in wcells:
            st = cells.get(c)
            if st is not None:
                if st[0] >= 0:
                    dep(st[0])
                for d in st[1].values():
                    dep(d)
        for c in rcells:
            st = cells.get(c)
            if st is None:
                cells[c] = [-1, {chan: op.opid}]
            else:
                st[1][chan] = op.opid
        for c in wcells:
            cells[c] = [op.opid, {}]
        op.edeps = edeps
        op.dwaits = dwaits
        if dma is not None:
            self.dma_cnt[dma] = self.dma_cnt.get(dma, 0) + 1
        ops.append(op)
        return op

    def finalize(self):
        ops = self.ops
        for op in ops:
            for d in op.edeps:
                ops[d].sig = True
        cnt = {e: 0 for e in ENGS}
        for op in ops:
            if op.sig:
                cnt[op.eng] += 1
                op.sigidx = cnt[op.eng]
        self.sigcnt = cnt

    def emit(self, block, sems, dma_sems, final_waits=None):
        ops = self.ops
        per_eng = {e: [] for e in ENGS}
        for op in ops:
            per_eng[op.eng].append(op)
        stats = {e: [len(per_eng[e]), 0] for e in ENGS}

        def run(eng, h):
            waited = {}
            nw = 0
            for op in per_eng[eng]:
                need = dict(op.dwaits)
                for d in op.edeps:
                    p = ops[d]
                    if need.get(p.eng, 0) < p.sigidx:
                        need[p.eng] = p.sigidx
                for ch, v in need.items():
                    if waited.get(ch, 0) < v:
                        s = sems[ch] if ch in sems else dma_sems[ch]
                        h.wait_ge(s, v)
                        waited[ch] = v
                        nw += 1
                ins = op.fn(h)
                if op.dma is not None:
                    ins.then_inc(dma_sems[op.dma], 16)
                elif op.sig:
                    ins.then_inc(sems[eng], 1)
            if final_waits and eng in final_waits:
                for ch in final_waits[eng]:
                    h.wait_ge(dma_sems[ch], self.dma_cnt[ch] * 16)
            stats[eng][1] = nw

        @block.tensor
        def _(h):
            run("pe", h)

        @block.scalar
        def _(h):
            run("act", h)

        @block.vector
        def _(h):
            run("dve", h)

        @block.gpsimd
        def _(h):
            run("pool", h)

        @block.sync
        def _(h):
            run("sp", h)

        return stats


D = 2048
NCH = 16
SEQ = 8192
NCORE = 8
TOK = SEQ // NCORE
HALO = 256
T = TOK + HALO
DFF = 5632
CFF = 256
NFF = DFF // CFF
PLE = 256
DEPTH = 4
EPS = 1e-6
NEG = -30000.0

C_NM, C_NF, C_NP, C_PB = 0, 16, 32, 48
C_CONV = 256
C_PSC = 448
C_GQ, C_GK, C_EPS, C_GQ8 = 464, 465, 466, 467
C_SINK = 468
NCOL = 512

DMA_SEMS = ["x", "cols", "misc", "f0", "f1", "c0", "c1", "g0", "g1", "g2", "g3", "wpp", "pt",
            "a0", "a1", "bias", "wdw", "pw", "o"]


def tiles_from(t0):
    out = []
    t = t0
    while t < T:
        n = min(512, T - t)
        out.append((t, n))
        t += n
    return out


def build_program(nlayers=DEPTH, dump_x=False):
    nc = bass.Bass("TRN2", target_bir_lowering=False)

    def dram(name, shape, kind="ExternalInput"):
        return nc.dram_tensor(name, list(shape), F32, kind=kind).ap()

    xT = dram("xT", [D, T])
    pT = dram("pT", [DEPTH, PLE, T])
    cols_d = dram("cols", [128, NCOL])
    tokmask_d = dram("tokmask", [128, HALO])
    kbias_d = dram("kbias", [128, 16])
    invc_d = dram("invc", [128, 64])
    ident_d = dram("ident", [128, 128])
    wdw_d = dram("wdw", [2, 128, 496])
    conv_w_in = dram("conv_w_in", [2, D, 2 * D])
    conv_w_out = dram("conv_w_out", [2, D, D])
    pool_w = dram("pool_w", [4, 512, 512])
    wq_d = dram("wq", [D, D])
    wk_d = dram("wk", [D, 512])
    wv_d = dram("wv", [D, 512])
    wo_d = dram("wo", [D, D])
    biasT_d = dram("biasT", [128, 32, 2, 128])
    maskT_d = dram("maskT", [128, 2, 128])
    wg_d = dram("ffn_w_gate", [DEPTH, D, DFF])
    wu_d = dram("ffn_w_up", [DEPTH, D, DFF])
    wd_d = dram("ffn_w_down", [DEPTH, DFF, D])
    wpp_d = dram("ple_w_proj", [DEPTH, PLE, D])
    wpg_d = dram("ple_w_gate", [DEPTH, D, D])
    yT = dram("yT", [D, T if dump_x else TOK], kind="ExternalOutput")

    P = Prog()
    es = ExitStack()
    with es:
        def sb(name, shape, dt, gran):
            t = es.enter_context(nc.sbuf_tensor(name, shape, dt))
            P.track(t, gran)
            return t

        X = sb("X", [128, NCH, T], F32, 128)
        A = sb("A", [128, NCH, T], BF16, 128)
        BW = sb("BW", [128, 28672], BF16, 128)
        S32 = sb("S32", [128, 3072], F32, 16)
        S16 = sb("S16", [128, 6656], BF16, 128)
        cols = sb("colsb", [128, NCOL], F32, NCOL)
        tokmask = sb("tokmaskb", [128, HALO], F32, HALO)
        kbias = sb("kbiasb", [128, 16], F32, 16)
        invc = sb("invcb", [128, 64], F32, 64)
        ones = sb("onesb", [128, 128], BF16, 128)
        bd = sb("bdb", [128, 128], BF16, 128)
        ident = sb("identb", [128, 128], BF16, 128)
        ps = []
        for i in range(8):
            t = es.enter_context(nc.psum_tensor(f"ps{i}", [128, 512], F32))
            P.track(t, 128)
            ps.append(t)
        sems = {e: es.enter_context(nc.semaphore(f"s_{e}")) for e in ENGS}
        dsems = {n: es.enter_context(nc.semaphore(f"d_{n}")) for n in DMA_SEMS}
        block = es.enter_context(nc.Block())

        bank_ctr = [0]

        def nb(lo=0, hi=8):
            k = bank_ctr[0]
            bank_ctr[0] += 1
            return lo + (k % (hi - lo))

        def aps(*xs):
            return [x for x in xs if not isinstance(x, (int, float)) and x is not None]

        def mm(out, lhsT, rhs, start, stop):
            P.add("pe", lambda h: h.matmul(out, lhsT=lhsT, rhs=rhs, start=start, stop=stop),
                  reads=[lhsT, rhs], writes=[out])

        def act(out, in_, func, bias=None, scale=None):
            kw = {}
            if bias is not None:
                kw["bias"] = bias
            if scale is not None:
                kw["scale"] = scale
            P.add("act", lambda h: h.activation(out=out, in_=in_, func=func, **kw),
                  reads=aps(in_, bias, scale), writes=[out])

        def amul(out, in_, m):
            P.add("act", lambda h: h.mul(out=out, in_=in_, mul=m), reads=aps(in_, m), writes=[out])

        def stt(out, in0, scalar, in1, op0, op1):
            P.add("dve", lambda h: h.scalar_tensor_tensor(out=out, in0=in0, scalar=scalar, in1=in1, op0=op0, op1=op1),
                  reads=aps(in0, scalar, in1), writes=[out])

        def ts(out, in0, s1, s2, op0, op1=None):
            if op1 is None:
                P.add("dve", lambda h: h.tensor_scalar(out=out, in0=in0, scalar1=s1, scalar2=None, op0=op0),
                      reads=aps(in0, s1), writes=[out])
            else:
                P.add("dve", lambda h: h.tensor_scalar(out=out, in0=in0, scalar1=s1, scalar2=s2, op0=op0, op1=op1),
                      reads=aps(in0, s1, s2), writes=[out])

        def tt(eng, out, in0, in1, op):
            P.add(eng, lambda h: h.tensor_tensor(out=out, in0=in0, in1=in1, op=op), reads=[in0, in1], writes=[out])

        def recip(out, in_):
            P.add("dve", lambda h: h.reciprocal(out=out, in_=in_), reads=[in_], writes=[out])

        def memset(eng, ap, v):
            P.add(eng, lambda h: h.memset(ap, v), writes=[ap])

        def dma(eng, out, in_, sem):
            P.add(eng, lambda h: h.dma_start(out=out, in_=in_), reads=[in_], writes=[out], dma=sem)

        def col(i):
            return cols[:, i:i + 1]

        dma("sp", cols[:], cols_d, "cols")
        dma("sp", tokmask[:], tokmask_d, "misc")
        dma("sp", kbias[:], kbias_d, "misc")
        dma("sp", invc[:], invc_d, "misc")
        dma("pool", ident[:], ident_d, "misc")
        xv = xT.rearrange("(c p) t -> p c t", p=128)
        for (t0, n) in tiles_from(0):
            dma("sp", X[:, :, t0:t0 + n], xv[:, :, t0:t0 + n], "x")
        memset("pool", ones[:], 1.0)
        memset("pool", bd[:], 0.0)
        memset("pool", bd[0:64, 0:64], 1.0)
        memset("pool", bd[64:128, 64:128], 1.0)
        amul(col(C_GQ8), col(C_GQ), 0.125)
        act(cols[:, C_SINK:C_SINK + 32], cols[:, C_SINK:C_SINK + 32], AF.Exp)

        sq_ctr = [0]

        def rmsnorm(gbase, tiles, mask_halo=False):
            for (t0, n) in tiles:
                bank = nb()
                for c4 in range(4):
                    sb_i = sq_ctr[0] % 2
                    sq_ctr[0] += 1
                    for cc in range(4):
                        c = c4 * 4 + cc
                        sq = S16[:, sb_i * 2048 + cc * 512: sb_i * 2048 + cc * 512 + n]
                        act(sq, X[:, c, t0:t0 + n], AF.Square)
                    for cc in range(4):
                        c = c4 * 4 + cc
                        sq = S16[:, sb_i * 2048 + cc * 512: sb_i * 2048 + cc * 512 + n]
                        mm(ps[bank][:, 0:n], ones[:], sq, c == 0, c == 15)
                rs = S32[:, 0:n]
                act(rs, ps[bank][:, 0:n], AF.Sqrt, bias=col(C_EPS), scale=1.0 / D)
                recip(rs, rs)
                if mask_halo and t0 < HALO:
                    m = min(HALO, t0 + n) - t0
                    tt("dve", S32[:, 0:m], S32[:, 0:m], tokmask[:, t0:t0 + m], ALU.mult)
                for c in range(NCH):
                    stt(A[:, c, t0:t0 + n], X[:, c, t0:t0 + n], col(gbase + c), rs, ALU.mult, ALU.mult)

        def ffn(li, tiles):
            rmsnorm(64 * li + C_NF, tiles)
            wgv = wg_d[li].rearrange("(kc p) n -> p kc n", p=128)
            wuv = wu_d[li].rearrange("(kc p) n -> p kc n", p=128)
            wdv = wd_d[li].rearrange("(s p) n -> p s n", p=128)

            def slot(s):
                base = s * 12288
                g = BW[:, base:base + 4096].rearrange("p (k n) -> p k n", k=16)
                u = BW[:, base + 4096:base + 8192].rearrange("p (k n) -> p k n", k=16)
                d = BW[:, base + 8192:base + 12288].rearrange("p (s n) -> p s n", s=2)
                return g, u, d

            def load(j):
                g, u, d = slot(j % 2)
                sem = f"f{j % 2}"
                dma("pool", g, wgv[:, :, j * CFF:(j + 1) * CFF], sem)
                dma("pool", u, wuv[:, :, j * CFF:(j + 1) * CFF], sem)
                dma("pool", d, wdv[:, 2 * j:2 * j + 2, :], sem)

            steps = [(j, ti) for j in range(NFF) for ti in range(len(tiles))]

            def gu(k):
                j, ti = steps[k]
                t0, n = tiles[ti]
                g, u, d = slot(j % 2)
                ab = k % 2
                for s in range(2):
                    bG, bU = 2 * s, 2 * s + 1
                    for kc in range(16):
                        mm(ps[bG][:, 0:n], g[:, kc, s * 128:(s + 1) * 128], A[:, kc, t0:t0 + n], kc == 0, kc == 15)
                    for kc in range(16):
                        mm(ps[bU][:, 0:n], u[:, kc, s * 128:(s + 1) * 128], A[:, kc, t0:t0 + n], kc == 0, kc == 15)
                    sg = S32[:, 1024 + s * 512: 1024 + s * 512 + n]
                    act(sg, ps[bG][:, 0:n], AF.Silu)
                    av = S16[:, 4096 + ab * 1024 + s * 512: 4096 + ab * 1024 + s * 512 + n]
                    tt("dve", av, ps[bU][:, 0:n], sg, ALU.mult)

            def down(k):
                j, ti = steps[k]
                t0, n = tiles[ti]
                g, u, d = slot(j % 2)
                ab = k % 2
                for m in range(NCH):
                    bD = nb(4, 8)
                    for s in range(2):
                        av = S16[:, 4096 + ab * 1024 + s * 512: 4096 + ab * 1024 + s * 512 + n]
                        mm(ps[bD][:, 0:n], d[:, s, m * 128:(m + 1) * 128], av, s == 0, s == 1)
                    tt("dve", X[:, m, t0:t0 + n], ps[bD][:, 0:n], X[:, m, t0:t0 + n], ALU.add)

            load(0)
            load(1)
            nt = len(tiles)
            gu(0)
            for k in range(len(steps)):
                if k + 1 < len(steps):
                    gu(k + 1)
                down(k)
                j, ti = steps[k]
                if ti == nt - 1 and j + 2 < NFF:
                    load(j + 2)

        def ple(li, tiles):
            rmsnorm(64 * li + C_NP, tiles)
            ptb = S16[:, 4096:4096 + 2 * T].rearrange("p (c t) -> p c t", c=2)
            dma("pool", ptb, pT[li].rearrange("(c p) t -> p c t", p=128), "pt")
            wpp = BW[:, 24576:28672].rearrange("p (c n) -> p c n", c=2)
            dma("pool", wpp, wpp_d[li].rearrange("(c p) n -> p c n", p=128), "wpp")
            wv = wpg_d[li].rearrange("(kc p) n -> p kc n", p=128)

            def slot(m):
                q = m % 4
                return BW[:, 16384 + q * 2048:16384 + (q + 1) * 2048].rearrange("p (k n) -> p k n", k=16)

            def load(m):
                dma("pool", slot(m), wv[:, :, m * 128:(m + 1) * 128], f"g{m % 4}")

            for m in range(4):
                load(m)
            for m in range(NCH):
                w = slot(m)
                for (t0, n) in tiles:
                    bG = nb()
                    for kc in range(16):
                        mm(ps[bG][:, 0:n], w[:, kc, :], A[:, kc, t0:t0 + n], kc == 0, kc == 15)
                    bP = nb()
                    for c in range(2):
                        mm(ps[bP][:, 0:n], wpp[:, c, m * 128:(m + 1) * 128], ptb[:, c, t0:t0 + n], c == 0, c == 1)
                    sg = S32[:, 1024:1024 + n]
                    act(sg, ps[bG][:, 0:n], AF.Sigmoid, bias=col(64 * li + C_PB + m))
                    tmp = S32[:, 1536:1536 + n]
                    tt("dve", tmp, ps[bP][:, 0:n], sg, ALU.mult)
                    tt("pool", X[:, m, t0:t0 + n], X[:, m, t0:t0 + n], tmp, ALU.add)
                if m + 4 < NCH:
                    load(m + 4)

        def conv_mixer(li, j, tiles_in, tiles_out):
            cb = C_CONV + 96 * j
            rmsnorm(64 * li + C_NM, tiles_in)
            wdw = S32[:, 2048:2048 + 496]
            dma("sp", wdw, wdw_d[j], "wdw")
            wiv = conv_w_in[j].rearrange("(kc p) n -> p kc n", p=128)
            Bv = BW[:, 0:NCH * T].rearrange("p (c t) -> p c t", c=NCH)
            ub = [S16[:, 0:T], S16[:, T:2 * T]]
            diag = S16[:, 2 * T:2 * T + 31 * 128].rearrange("p (k n) -> p k n", k=31)
            tin0 = tiles_in[0][0]
            tout0 = tiles_out[0][0]

            def slot(m):
                base = 20480 + (m % 2) * 4096
                a = BW[:, base:base + 2048].rearrange("p (k n) -> p k n", k=16)
                g = BW[:, base + 2048:base + 4096].rearrange("p (k n) -> p k n", k=16)
                return a, g

            def load(m):
                a, g = slot(m)
                dma("pool", a, wiv[:, :, m * 128:(m + 1) * 128], f"c{m % 2}")
                dma("pool", g, wiv[:, :, D + m * 128:D + (m + 1) * 128], f"c{m % 2}")

            def inproj(m):
                a, g = slot(m)
                u = ub[m % 2]
                for (t0, n) in tiles_in:
                    bA = nb()
                    for kc in range(16):
                        mm(ps[bA][:, 0:n], a[:, kc, :], A[:, kc, t0:t0 + n], kc == 0, kc == 15)
                    bG = nb()
                    for kc in range(16):
                        mm(ps[bG][:, 0:n], g[:, kc, :], A[:, kc, t0:t0 + n], kc == 0, kc == 15)
                    sgb = (t0 // 512) % 2
                    sg = S32[:, 1024 + sgb * 512:1024 + sgb * 512 + n]
                    act(sg, ps[bG][:, 0:n], AF.Sigmoid, bias=col(cb + 16 + m))
                    if t0 < HALO:
                        mlen = min(HALO, t0 + n) - t0
                        tt("pool", sg[:, 0:mlen], sg[:, 0:mlen], tokmask[:, t0:t0 + mlen], ALU.mult)
                    stt(u[:, t0:t0 + n], ps[bA][:, 0:n], col(cb + m), sg, ALU.add, ALU.mult)

            def dwconv(m):
                u = ub[m % 2]
                for k in range(31):
                    amul(diag[:, k, :], ident[:], wdw[:, k * 16 + m:k * 16 + m + 1])
                for (t0, n) in tiles_out:
                    o0 = max(t0, tin0 + 32)
                    nn = t0 + n - o0
                    bC = nb()
                    for k in range(31):
                        mm(ps[bC][:, 0:nn], diag[:, k, :], u[:, o0 - 30 + k:o0 - 30 + k + nn], k == 0, k == 30)
                    act(Bv[:, m, o0:o0 + nn], ps[bC][:, 0:nn], AF.Identity, bias=col(cb + 32 + m))

            load(0)
            load(1)
            if tout0 < tin0 + 32:
                memset("pool", Bv[:, :, tout0:tin0 + 32], 0.0)
            for m in range(NCH + 1):
                if m < NCH:
                    inproj(m)
                if m >= 1:
                    dwconv(m - 1)
                    if m + 1 < NCH:
                        load(m + 1)
            for (t0, n) in tiles_out:
                b1 = nb()
                b2 = nb()
                for c4 in range(4):
                    sb_i = sq_ctr[0] % 2
                    sq_ctr[0] += 1
                    for cc in range(4):
                        c = c4 * 4 + cc
                        sq = S16[:, sb_i * 2048 + cc * 512: sb_i * 2048 + cc * 512 + n]
                        act(sq, Bv[:, c, t0:t0 + n], AF.Square)
                    for cc in range(4):
                        c = c4 * 4 + cc
                        sq = S16[:, sb_i * 2048 + cc * 512: sb_i * 2048 + cc * 512 + n]
                        mm(ps[b1][:, 0:n], ones[:], Bv[:, c, t0:t0 + n], c == 0, c == 15)
                        mm(ps[b2][:, 0:n], ones[:], sq, c == 0, c == 15)
                mean = S32[:, 0:n]
                rstd = S32[:, 512:512 + n]
                msq = S32[:, 1024:1024 + n]
                act(mean, ps[b1][:, 0:n], AF.Identity, scale=1.0 / D)
                tt("dve", msq, mean, mean, ALU.mult)
                stt(rstd, ps[b2][:, 0:n], 1.0 / D, msq, ALU.mult, ALU.subtract)
                ts(rstd, rstd, 0.0, None, ALU.max)
                act(rstd, rstd, AF.Sqrt, bias=col(C_EPS))
                recip(rstd, rstd)
                for c in range(NCH):
                    t1 = S32[:, 2560:2560 + n] if c % 2 else S32[:, 1536:1536 + n]
                    tt("dve", t1, Bv[:, c, t0:t0 + n], mean, ALU.subtract)
                    tt("dve", t1, t1, rstd, ALU.mult)
                    act(A[:, c, t0:t0 + n], t1, AF.Silu, bias=col(cb + 64 + c), scale=col(cb + 48 + c))
            wov = conv_w_out[j].rearrange("(kc p) n -> p kc n", p=128)

            def oslot(m2):
                base = 20480 + (m2 % 2) * 4096
                return BW[:, base:base + 4096].rearrange("p (k n) -> p k n", k=16)

            def oload(m2):
                dma("pool", oslot(m2), wov[:, :, m2 * 256:(m2 + 1) * 256], f"c{m2 % 2}")

            oload(0)
            oload(1)
            for m2 in range(8):
                w = oslot(m2)
                for mi in range(2):
                    m = m2 * 2 + mi
                    for (t0, n) in tiles_out:
                        bO = nb()
                        for kc in range(16):
                            mm(ps[bO][:, 0:n], w[:, kc, mi * 128:(mi + 1) * 128], A[:, kc, t0:t0 + n], kc == 0, kc == 15)
                        stt(X[:, m, t0:t0 + n], ps[bO][:, 0:n], col(cb + 80 + m), X[:, m, t0:t0 + n], ALU.add, ALU.add)
                if m2 + 2 < 8:
                    oload(m2 + 2)

        def pool_mixer(li, tiles):
            rmsnorm(64 * li + C_NM, tiles, mask_halo=True)
            Bv = BW[:, 0:NCH * T].rearrange("p (c t) -> p c t", c=NCH)
            wp = BW[:, 20480:28672].rearrange("p (g k n) -> p g k n", g=4, k=4)
            for g in range(4):
                dma("pool", wp[:, g], pool_w[g].rearrange("(k p) n -> p k n", p=128), "pw")
            memset("pool", Bv[:, :, 0:16], 0.0)
            for c in range(NCH):
                g = c // 4
                L = g + 1
                w = 2 ** L
                base = (c % 2) * 1536
                for (o0, o1) in ((16, 656), (656, T)):
                    i0 = o0 - 16
                    ln = o1 - i0
                    tA = S32[:, base:base + ln]
                    tB = S32[:, base + 672:base + 672 + ln]
                    src = A[:, c, i0:o1]
                    cur = src
                    bufs = [tA, tB]
                    for l in range(1, L + 1):
                        sh = 2 ** (l - 1)
                        lo = 2 ** l - 1
                        dst = bufs[(l - 1) % 2]
                        tt("pool", dst[:, lo:ln], cur[:, lo:ln], cur[:, lo - sh:ln - sh], ALU.add)
                        cur = dst
                    stt(Bv[:, c, o0:o1], cur[:, 16:ln], 1.0 / w, A[:, c, o0:o1], ALU.mult, ALU.subtract)
                    if o0 <= HALO < o1:
                        r0 = HALO - i0
                        t16 = S32[:, base + 1344:base + 1360]
                        tt("dve", t16, cur[:, r0:r0 + 16], invc[:, g * 16:(g + 1) * 16], ALU.mult)
                        tt("dve", Bv[:, c, HALO:HALO + 16], t16, A[:, c, HALO:HALO + 16], ALU.subtract)
            for g in range(4):
                for mo in range(4):
                    m = g * 4 + mo
                    for (t0, n) in tiles:
                        bO = nb()
                        for kc in range(4):
                            mm(ps[bO][:, 0:n], wp[:, g, kc, mo * 128:(mo + 1) * 128], Bv[:, g * 4 + kc, t0:t0 + n], kc == 0, kc == 3)
                        stt(X[:, m, t0:t0 + n], ps[bO][:, 0:n], col(C_PSC + m), X[:, m, t0:t0 + n], ALU.mult, ALU.add)

        def attn_mixer(li, tiles_q):
            rmsnorm(64 * li + C_NM, tiles_from(0))
            TQ = T - 128
            Qv = BW[:, 0:NCH * TQ].rearrange("p (c t) -> p c t", c=NCH)
            Kv = BW[:, 18432:18432 + 4 * T].rearrange("p (c t) -> p c t", c=4)
            Vv = BW[:, 23552:23552 + 10 * 512].rearrange("p (b n) -> p b n", b=10)
            Ov = A[:, :, 0:TQ]

            def ws(i):
                b = 2048 + (i % 2) * 2048
                return S16[:, b:b + 2048].rearrange("p (k n) -> p k n", k=16)

            widx = [0]

            def load_w(src_ap):
                i = widx[0]
                widx[0] += 1
                w = ws(i)
                dma("pool", w, src_ap, f"a{i % 2}")
                return w

            def qknorm(psb, n, gcol, dst):
                sq = S16[:, 0:n] if (sq_ctr[0] % 2 == 0) else S16[:, 512:512 + n]
                sq_ctr[0] += 1
                act(sq, ps[psb][:, 0:n], AF.Square)
                b2 = nb()
                mm(ps[b2][:, 0:n], bd[:], sq, True, True)
                rs = S32[:, 0:n]
                act(rs, ps[b2][:, 0:n], AF.Sqrt, bias=col(C_EPS), scale=1.0 / 64)
                recip(rs, rs)
                stt(dst, ps[psb][:, 0:n], col(gcol), rs, ALU.mult, ALU.mult)

            wkv = wk_d.rearrange("(kc p) n -> p kc n", p=128)
            wvv = wv_d.rearrange("(kc p) n -> p kc n", p=128)
            wqv = wq_d.rearrange("(kc p) n -> p kc n", p=128)
            wov = wo_d.rearrange("(kc p) n -> p kc n", p=128)
            for Pp in range(4):
                w = load_w(wkv[:, :, Pp * 128:(Pp + 1) * 128])
                for (t0, n) in tiles_from(0):
                    b = nb()
                    for kc in range(16):
                        mm(ps[b][:, 0:n], w[:, kc, :], A[:, kc, t0:t0 + n], kc == 0, kc == 15)
                    qknorm(b, n, C_GK, Kv[:, Pp, t0:t0 + n])
            for Pp in range(4):
                w = load_w(wvv[:, :, Pp * 128:(Pp + 1) * 128])
                for bl in range(10):
                    b = nb()
                    for kc in range(16):
                        mm(ps[b][:, 0:128], A[:, kc, bl * 128:(bl + 1) * 128], w[:, kc, :], kc == 0, kc == 15)
                    act(Vv[:, bl, Pp * 128:(Pp + 1) * 128], ps[b][:, 0:128], AF.Identity)
            for qc in range(NCH):
                w = load_w(wqv[:, :, qc * 128:(qc + 1) * 128])
                for (t0, n) in tiles_q:
                    b = nb()
                    for kc in range(16):
                        mm(ps[b][:, 0:n], w[:, kc, :], A[:, kc, t0:t0 + n], kc == 0, kc == 15)
                    qknorm(b, n, C_GQ8, Qv[:, qc, t0 - 128:t0 - 128 + n])
            maskb = S32[:, 1024:1280].rearrange("p (k q) -> p k q", k=2)
            dma("sp", maskb, maskT_d, "bias")
            biasm = S32[:, 0:1024].rearrange("p (g k q) -> p g k q", g=4, k=2)
            for kv in range(8):
                Pp, half = kv // 2, kv % 2
                pr = slice(half * 64, half * 64 + 64)
                dma("sp", biasm, biasT_d[:, kv * 4:(kv + 1) * 4], "bias")
                for g in range(4):
                    tt("pool", biasm[:, g], biasm[:, g], maskb, ALU.add)
                for b in range(1, 10):
                    eb = (b % 2) * 1024
                    for kc in range(2):
                        kb = b - 1 + kc
                        bS = nb()
                        for g in range(4):
                            mm(ps[bS][:, g * 128:(g + 1) * 128], Kv[pr, Pp, kb * 128:(kb + 1) * 128],
                               Qv[pr, Pp * 4 + g, (b - 1) * 128:b * 128], True, True)
                        tbuf = S32[:, 1536 + kc * 512:1536 + (kc + 1) * 512]
                        tt("dve", tbuf.rearrange("p (g q) -> p g q", g=4), ps[bS][:, 0:512].rearrange("p (g q) -> p g q", g=4),
                           biasm[:, :, kc, :], ALU.add)
                        e = S16[:, eb + kc * 512:eb + (kc + 1) * 512]
                        act(e, tbuf, AF.Exp, bias=kbias[:, kb:kb + 1])
                    bO = nb()
                    bDn = nb()
                    for kc in range(2):
                        kb = b - 1 + kc
                        e = S16[:, eb + kc * 512:eb + (kc + 1) * 512]
                        mm(ps[bO][:, 0:512], Vv[:, kb, Pp * 128:(Pp + 1) * 128], e, kc == 0, kc == 1)
                    for kc in range(2):
                        e = S16[:, eb + kc * 512:eb + (kc + 1) * 512]
                        mm(ps[bDn][:, 0:512], ones[:], e, kc == 0, kc == 1)
                    rden = S32[:, 2560:3072]
                    for g in range(4):
                        ts(rden[pr, g * 128:(g + 1) * 128], ps[bDn][pr, g * 128:(g + 1) * 128], col(C_SINK + kv * 4 + g)[pr], None, ALU.add)
                    recip(rden[pr], rden[pr])
                    tt("dve", Ov[pr, Pp * 4:Pp * 4 + 4, (b - 1) * 128:b * 128],
                       ps[bO][pr, 0:512].rearrange("p (g q) -> p g q", g=4), rden[pr].rearrange("p (g q) -> p g q", g=4), ALU.mult)
            for m in range(NCH):
                w = load_w(wov[:, :, m * 128:(m + 1) * 128])
                for (t0, n) in tiles_q:
                    b = nb()
                    for kc in range(16):
                        mm(ps[b][:, 0:n], w[:, kc, :], Ov[:, kc, t0 - 128:t0 - 128 + n], kc == 0, kc == 15)
                    tt("dve", X[:, m, t0:t0 + n], ps[b][:, 0:n], X[:, m, t0:t0 + n], ALU.add)

        for li in range(nlayers):
            kind, j = li % 3, li // 3
            if li < 2:
                tl = tiles_from(0)
            elif li == 2:
                tl = tiles_from(128)
            else:
                tl = tiles_from(256)
            if kind == 0:
                if li == 0:
                    conv_mixer(li, j, tiles_from(0), tiles_from(0))
                else:
                    conv_mixer(li, j, tiles_from(128), tiles_from(256))
            elif kind == 1:
                pool_mixer(li, tl)
            else:
                attn_mixer(li, tl)
            ffn(li, tl)
            ple(li, tl)

        yv = yT.rearrange("(c p) t -> p c t", p=128)
        if dump_x:
            dma("sp", yv, X[:], "o")
        else:
            dma("sp", yv, X[:, :, HALO:T], "o")
        P.finalize()
        stats = P.emit(block, sems, dsems, final_waits={"sp": ["o"]})
    return nc, stats


def _t5_bucket(rel):
    nb_ = 16
    n = -rel
    ret = np.where(n < 0, nb_, 0)
    n = np.abs(n)
    max_exact = nb_ // 2
    nf = np.maximum(n, 1).astype(np.float32)
    large = max_exact + (np.log(nf / max_exact) / math.log(128 / max_exact) * (nb_ - max_exact)).astype(np.int32)
    large = np.minimum(large, nb_ - 1)
    return ret + np.where(n < max_exact, n, large)


def _colpack(v):
    v = np.asarray(v, np.float32)
    return np.ascontiguousarray(v.reshape(-1, 128).T)


def prepare_inputs(inp):
    f32 = np.float32
    x = np.asarray(inp["x"], f32)[0]
    p = np.asarray(inp["p"], f32)[:, 0]
    cols = np.zeros((128, NCOL), f32)
    for li in range(DEPTH):
        cols[:, 64 * li + C_NM:64 * li + C_NM + 16] = _colpack(inp["norm_mix"][li])
        cols[:, 64 * li + C_NF:64 * li + C_NF + 16] = _colpack(inp["norm_ffn"][li])
        cols[:, 64 * li + C_NP:64 * li + C_NP + 16] = _colpack(inp["norm_ple"][li])
        cols[:, 64 * li + C_PB:64 * li + C_PB + 16] = _colpack(inp["ple_b_gate"][li])
    for j in range(2):
        cb = C_CONV + 96 * j
        cols[:, cb:cb + 32] = _colpack(inp["conv_b_in"][j])
        cols[:, cb + 32:cb + 48] = _colpack(inp["conv_b_dw"][j])
        cols[:, cb + 48:cb + 64] = _colpack(inp["conv_ln_g"][j])
        cols[:, cb + 64:cb + 80] = _colpack(inp["conv_ln_b"][j])
        cols[:, cb + 80:cb + 96] = _colpack(inp["conv_b_out"][j])
    cols[:, C_PSC:C_PSC + 16] = _colpack(inp["pool_scale"][0])
    cols[:, C_GQ] = np.tile(np.asarray(inp["attn_q_norm"][0], f32), 2)
    cols[:, C_GK] = np.tile(np.asarray(inp["attn_k_norm"][0], f32), 2)
    cols[:, C_EPS] = EPS
    cols[:, C_SINK:C_SINK + 32] = np.asarray(inp["attn_sinks"][0], f32)[None, :]
    wdw = np.zeros((2, 128, 496), f32)
    for j in range(2):
        w = np.asarray(inp["conv_w_dw"][j], f32)
        wdw[j] = w.reshape(31, 16, 128).transpose(2, 0, 1).reshape(128, 496)
    wqkv = np.asarray(inp["attn_w_qkv"][0], f32)
    wq = wqkv[:, :2048]
    perm = np.zeros(2048, np.int64)
    for Pp in range(4):
        for g in range(4):
            for half in range(2):
                h = (2 * Pp + half) * 4 + g
                qc = Pp * 4 + g
                perm[qc * 128 + half * 64:qc * 128 + half * 64 + 64] = np.arange(h * 64, h * 64 + 64)
    wq_p = np.ascontiguousarray(wq[:, perm])
    wk = np.ascontiguousarray(wqkv[:, 2048:2560])
    wv = np.ascontiguousarray(wqkv[:, 2560:3072])
    wo_p = np.ascontiguousarray(np.asarray(inp["attn_w_o"][0], f32)[perm, :])
    rb = np.asarray(inp["rel_bias"], f32)
    i = np.arange(128)[None, :]
    biasT = np.zeros((128, 32, 2, 128), f32)
    maskT = np.zeros((128, 2, 128), f32)
    for kc in range(2):
        jj = np.arange(128)[:, None] + 128 * kc
        rel = jj - 128 - i
        bk = _t5_bucket(rel)
        biasT[:, :, kc, :] = rb[bk].transpose(0, 2, 1)
        qc_ = i // 64
        kc_ = np.floor_divide(jj - 128, 64)
        ok = (kc_ <= qc_) & (kc_ >= qc_ - 2)
        maskT[:, kc, :] = np.where(ok, 0.0, NEG)
    ident = np.eye(128, dtype=f32)
    shared = {
        "cols": cols, "ident": ident, "wdw": wdw,
        "conv_w_in": np.asarray(inp["conv_w_in"], f32), "conv_w_out": np.asarray(inp["conv_w_out"], f32),
        "pool_w": np.asarray(inp["pool_w"], f32)[0],
        "wq": wq_p, "wk": wk, "wv": wv, "wo": wo_p, "biasT": biasT, "maskT": maskT,
        "ffn_w_gate": np.asarray(inp["ffn_w_gate"], f32), "ffn_w_up": np.asarray(inp["ffn_w_up"], f32),
        "ffn_w_down": np.asarray(inp["ffn_w_down"], f32),
        "ple_w_proj": np.asarray(inp["ple_w_proj"], f32), "ple_w_gate": np.asarray(inp["ple_w_gate"], f32),
    }
    in_maps = []
    for c in range(NCORE):
        T0 = c * TOK
        lo = T0 - HALO
        xT = np.zeros((D, T), f32)
        pTc = np.zeros((DEPTH, PLE, T), f32)
        s = max(lo, 0)
        xT[:, s - lo:] = x[s:T0 + TOK].T
        pTc[:, :, s - lo:] = p[:, s:T0 + TOK].transpose(0, 2, 1)
        gpos = lo + np.arange(HALO)
        tokmask = np.broadcast_to((gpos >= 0).astype(f32)[None, :], (128, HALO)).copy()
        kb = np.zeros((128, 16), f32)
        for b in range(10):
            if lo + 128 * b < 0:
                kb[:, b] = NEG
        invc = np.zeros((128, 64), f32)
        for g in range(4):
            w = 2 ** (g + 1)
            gt = T0 + np.arange(16)
            cnt = np.minimum(gt + 1, w).astype(f32)
            invc[:, g * 16:(g + 1) * 16] = (1.0 / cnt)[None, :]
        m = dict(shared)
        m.update({"xT": xT, "pT": pTc, "tokmask": tokmask, "kbias": kb, "invc": invc})
        in_maps.append(m)
    return in_maps


_CACHE = {}


def kernel(**inputs):
    if "nc" not in _CACHE:
        _CACHE["nc"] = build_program()[0]
    nc = _CACHE["nc"]
    in_maps = prepare_inputs(inputs)
    res = run_bass_kernel_spmd(nc, in_maps, core_ids=list(range(NCORE)))
    out = np.zeros((1, SEQ, D), np.float32)
    for c in range(NCORE):
        out[0, c * TOK:(c + 1) * TOK, :] = res.results[c]["yT"].T
    return out
```

```python
import math
from contextlib import ExitStack

import numpy as np
import concourse.bass as bass
import concourse.mybir as mybir
from concourse.bass_utils import run_bass_kernel_spmd

F32 = mybir.dt.float32
BF16 = mybir.dt.bfloat16
AF = mybir.ActivationFunctionType
ALU = mybir.AluOpType

ENGS = ("pe", "act", "dve", "pool", "sp")


class Op:
    __slots__ = ("eng", "fn", "edeps", "dwaits", "dma", "sig", "sigidx", "opid")


class Prog:
    def __init__(self, same_engine_sync=True):
        self.ops = []
        self.cells = {}
        self.gran = {}
        self.rowsz = {}
        self.dma_cnt = {}
        self.cell_cache = {}
        self.same_engine_sync = same_engine_sync

    def track(self, handle, gran):
        row = 1
        for s in list(handle.shape)[1:]:
            row *= s
        self.gran[handle.name] = gran
        self.rowsz[handle.name] = row

    def cells_of(self, ap):
        name = ap.name
        g = self.gran.get(name)
        if g is None:
            return ()
        key = (name, ap.offset, ap.ap)
        r = self.cell_cache.get(key)
        if r is not None:
            return r
        row = self.rowsz[name]
        off = ap.offset
        dims = ap.ap
        pcnt = dims[0][1]
        p0 = off // row
        foff = off % row
        p1 = p0 + pcnt
        halves = []
        if p0 < 64:
            halves.append(0)
        if p1 > 64:
            halves.append(1)
        starts = [foff]
        free = dims[1:]
        for (st, cnt) in free[:-1]:
            starts = [s + i * st for s in starts for i in range(cnt)]
        if free:
            lst, lcnt = free[-1]
            ext = (lcnt - 1) * abs(lst) + 1
        else:
            ext = 1
        cs = set()
        for s in starts:
            for c in range(s // g, (s + ext - 1) // g + 1):
                for h in halves:
                    cs.add((name, h, c))
        r = tuple(cs)
        self.cell_cache[key] = r
        return r

    def add(self, eng, fn, reads=(), writes=(), dma=None):
        op = Op()
        op.eng = eng
        op.fn = fn
        op.dma = dma
        op.sig = False
        op.sigidx = 0
        op.opid = len(self.ops)
        chan = dma if dma is not None else eng
        edeps = set()
        dwaits = {}
        cells = self.cells
        ops = self.ops
        rcells = []
        wcells = []
        for ap in reads:
            rcells.extend(self.cells_of(ap))
        for ap in writes:
            wcells.extend(self.cells_of(ap))

        def dep(d):
            p = ops[d]
            if p.dma is not None:
                v = self.dma_cnt[p.dma] * 16
                if dwaits.get(p.dma, 0) < v:
                    dwaits[p.dma] = v
            else:
                if p.eng == eng and dma is None:
                    if eng == "pe" or not self.same_engine_sync:
                        return
                edeps.add(d)

        for c in rcells:
            st = cells.get(c)
            if st is not None and st[0] >= 0:
                dep(st[0])
        for c in wcells:
            st = cells.get(c)
            if st is not None:
                if st[0] >= 0:
                    dep(st[0])
                for d in st[1].values():
                    dep(d)
        for c in rcells:
            st = cells.get(c)
            if st is None:
                cells[c] = [-1, {chan: op.opid}]
            else:
                st[1][chan] = op.opid
        for c in wcells:
            cells[c] = [op.opid, {}]
        op.edeps = edeps
        op.dwaits = dwaits
        if dma is not None:
            self.dma_cnt[dma] = self.dma_cnt.get(dma, 0) + 1
        ops.append(op)
        return op

    def finalize(self):
        ops = self.ops
        for op in ops:
            for d in op.edeps:
                ops[d].sig = True
        cnt = {e: 0 for e in ENGS}
        for op in ops:
            if op.sig:
                cnt[op.eng] += 1
                op.sigidx = cnt[op.eng]
        self.sigcnt = cnt

    def emit(self, block, sems, dma_sems, final_waits=None):
        ops = self.ops
        per_eng = {e: [] for e in ENGS}
        for op in ops:
            per_eng[op.eng].append(op)
        stats = {e: [len(per_eng[e]), 0] for e in ENGS}

        def run(eng, h):
            waited = {}
            nw = 0
            for op in per_eng[eng]:
                need = dict(op.dwaits)
                for d in op.edeps:
                    p = ops[d]
                    if need.get(p.eng, 0) < p.sigidx:
                        need[p.eng] = p.sigidx
                for ch, v in need.items():
                    if waited.get(ch, 0) < v:
                        s = sems[ch] if ch in sems else dma_sems[ch]
                        h.wait_ge(s, v)
                        waited[ch] = v
                        nw += 1
                ins = op.fn(h)
                if op.dma is not None:
                    ins.then_inc(dma_sems[op.dma], 16)
                elif op.sig:
                    ins.then_inc(sems[eng], 1)
            if final_waits and eng in final_waits:
                for ch in final_waits[eng]:
                    h.wait_ge(dma_sems[ch], self.dma_cnt[ch] * 16)
            stats[eng][1] = nw

        @block.tensor
        def _(h):
            run("pe", h)

        @block.scalar
        def _(h):
            run("act", h)

        @block.vector
        def _(h):
            run("dve", h)

        @block.gpsimd
        def _(h):
            run("pool", h)

        @block.sync
        def _(h):
            run("sp", h)

        return stats


D = 2048
NCH = 16
SEQ = 8192
NCORE = 8
TOK = SEQ // NCORE
HALO = 256
T = TOK + HALO
DFF = 5632
CFF = 256
NFF = DFF // CFF
PLE = 256
DEPTH = 4
EPS = 1e-6
NEG = -30000.0

C_NM, C_NF, C_NP, C_PB = 0, 16, 32, 48
C_CONV = 256
C_PSC = 448
C_GQ, C_GK, C_EPS, C_GQ8 = 464, 465, 466, 467
C_SINK = 468
NCOL = 512

DMA_SEMS = ["x", "cols", "misc", "f0", "f1", "c0", "c1", "g0", "g1", "g2", "g3", "wpp", "pt",
            "a0", "a1", "bias", "wdw", "pw", "o"]


def tiles_from(t0):
    out = []
    t = t0
    while t < T:
        n = min(512, T - t)
        out.append((t, n))
        t += n
    return out


def build_program(nlayers=DEPTH, dump_x=False):
    nc = bass.Bass("TRN2", target_bir_lowering=False)

    def dram(name, shape, kind="ExternalInput"):
        return nc.dram_tensor(name, list(shape), F32, kind=kind).ap()

    xT = dram("xT", [D, T])
    pT = dram("pT", [DEPTH, PLE, T])
    cols_d = dram("cols", [128, NCOL])
    tokmask_d = dram("tokmask", [128, HALO])
    kbias_d = dram("kbias", [128, 16])
    invc_d = dram("invc", [128, 64])
    ident_d = dram("ident", [128, 128])
    wdw_d = dram("wdw", [2, 128, 496])
    conv_w_in = dram("conv_w_in", [2, D, 2 * D])
    conv_w_out = dram("conv_w_out", [2, D, D])
    pool_w = dram("pool_w", [4, 512, 512])
    wq_d = dram("wq", [D, D])
    wk_d = dram("wk", [D, 512])
    wv_d = dram("wv", [D, 512])
    wo_d = dram("wo", [D, D])
    biasT_d = dram("biasT", [128, 32, 2, 128])
    maskT_d = dram("maskT", [128, 2, 128])
    wg_d = dram("ffn_w_gate", [DEPTH, D, DFF])
    wu_d = dram("ffn_w_up", [DEPTH, D, DFF])
    wd_d = dram("ffn_w_down", [DEPTH, DFF, D])
    wpp_d = dram("ple_w_proj", [DEPTH, PLE, D])
    wpg_d = dram("ple_w_gate", [DEPTH, D, D])
    yT = dram("yT", [D, T if dump_x else TOK], kind="ExternalOutput")

    P = Prog()
    es = ExitStack()
    with es:
        def sb(name, shape, dt, gran):
            t = es.enter_context(nc.sbuf_tensor(name, shape, dt))
            P.track(t, gran)
            return t

        X = sb("X", [128, NCH, T], F32, 128)
        A = sb("A", [128, NCH, T], BF16, 128)
        BW = sb("BW", [128, 28672], BF16, 128)
        S32 = sb("S32", [128, 3072], F32, 16)
        S16 = sb("S16", [128, 6656], BF16, 128)
        cols = sb("colsb", [128, NCOL], F32, NCOL)
        tokmask = sb("tokmaskb", [128, HALO], F32, HALO)
        kbias = sb("kbiasb", [128, 16], F32, 16)
        invc = sb("invcb", [128, 64], F32, 64)
        ones = sb("onesb", [128, 128], BF16, 128)
        bd = sb("bdb", [128, 128], BF16, 128)
        ident = sb("identb", [128, 128], BF16, 128)
        ps = []
        for i in range(8):
            t = es.enter_context(nc.psum_tensor(f"ps{i}", [128, 512], F32))
            P.track(t, 128)
            ps.append(t)
        sems = {e: es.enter_context(nc.semaphore(f"s_{e}")) for e in ENGS}
        dsems = {n: es.enter_context(nc.semaphore(f"d_{n}")) for n in DMA_SEMS}
        block = es.enter_context(nc.Block())

        bank_ctr = [0]

        def nb(lo=0, hi=8):
            k = bank_ctr[0]
            bank_ctr[0] += 1
            return lo + (k % (hi - lo))

        def aps(*xs):
            return [x for x in xs if not isinstance(x, (int, float)) and x is not None]

        def mm(out, lhsT, rhs, start, stop):
            P.add("pe", lambda h: h.matmul(out, lhsT=lhsT, rhs=rhs, start=start, stop=stop),
                  reads=[lhsT, rhs], writes=[out])

        def act(out, in_, func, bias=None, scale=None):
            kw = {}
            if bias is not None:
                kw["bias"] = bias
            if scale is not None:
                kw["scale"] = scale
            P.add("act", lambda h: h.activation(out=out, in_=in_, func=func, **kw),
                  reads=aps(in_, bias, scale), writes=[out])

        def amul(out, in_, m):
            P.add("act", lambda h: h.mul(out=out, in_=in_, mul=m), reads=aps(in_, m), writes=[out])

        def stt(out, in0, scalar, in1, op0, op1):
            P.add("dve", lambda h: h.scalar_tensor_tensor(out=out, in0=in0, scalar=scalar, in1=in1, op0=op0, op1=op1),
                  reads=aps(in0, scalar, in1), writes=[out])

        def ts(out, in0, s1, s2, op0, op1=None):
            if op1 is None:
                P.add("dve", lambda h: h.tensor_scalar(out=out, in0=in0, scalar1=s1, scalar2=None, op0=op0),
                      reads=aps(in0, s1), writes=[out])
            else:
                P.add("dve", lambda h: h.tensor_scalar(out=out, in0=in0, scalar1=s1, scalar2=s2, op0=op0, op1=op1),
                      reads=aps(in0, s1, s2), writes=[out])

        def tt(eng, out, in0, in1, op):
            P.add(eng, lambda h: h.tensor_tensor(out=out, in0=in0, in1=in1, op=op), reads=[in0, in1], writes=[out])

        def recip(out, in_):
            P.add("dve", lambda h: h.reciprocal(out=out, in_=in_), reads=[in_], writes=[out])

        def memset(eng, ap, v):
            P.add(eng, lambda h: h.memset(ap, v), writes=[ap])

        def dma(eng, out, in_, sem):
            P.add(eng, lambda h: h.dma_start(out=out, in_=in_), reads=[in_], writes=[out], dma=sem)

        def col(i):
            return cols[:, i:i + 1]

        dma("sp", cols[:], cols_d, "cols")
        dma("sp", tokmask[:], tokmask_d, "misc")
        dma("sp", kbias[:], kbias_d, "misc")
        dma("sp", invc[:], invc_d, "misc")
        dma("pool", ident[:], ident_d, "misc")
        xv = xT.rearrange("(c p) t -> p c t", p=128)
        for (t0, n) in tiles_from(0):
            dma("sp", X[:, :, t0:t0 + n], xv[:, :, t0:t0 + n], "x")
        memset("pool", ones[:], 1.0)
        memset("pool", bd[:], 0.0)
        memset("pool", bd[0:64, 0:64], 1.0)
        memset("pool", bd[64:128, 64:128], 1.0)
        amul(col(C_GQ8), col(C_GQ), 0.125)
        act(cols[:, C_SINK:C_SINK + 32], cols[:, C_SINK:C_SINK + 32], AF.Exp)

        sq_ctr = [0]

        def rmsnorm(gbase, tiles, mask_halo=False):
            for (t0, n) in tiles:
                bank = nb()
                for c4 in range(4):
                    sb_i = sq_ctr[0] % 2
                    sq_ctr[0] += 1
                    for cc in range(4):
                        c = c4 * 4 + cc
                        sq = S16[:, sb_i * 2048 + cc * 512: sb_i * 2048 + cc * 512 + n]
                        act(sq, X[:, c, t0:t0 + n], AF.Square)
                    for cc in range(4):
                        c = c4 * 4 + cc
                        sq = S16[:, sb_i * 2048 + cc * 512: sb_i * 2048 + cc * 512 + n]
                        mm(ps[bank][:, 0:n], ones[:], sq, c == 0, c == 15)
                rs = S32[:, 0:n]
                act(rs, ps[bank][:, 0:n], AF.Sqrt, bias=col(C_EPS), scale=1.0 / D)
                recip(rs, rs)
                if mask_halo and t0 < HALO:
                    m = min(HALO, t0 + n) - t0
                    tt("dve", S32[:, 0:m], S32[:, 0:m], tokmask[:, t0:t0 + m], ALU.mult)
                for c in range(NCH):
                    stt(A[:, c, t0:t0 + n], X[:, c, t0:t0 + n], col(gbase + c), rs, ALU.mult, ALU.mult)

        def ffn(li, tiles):
            rmsnorm(64 * li + C_NF, tiles)
            wgv = wg_d[li].rearrange("(kc p) n -> p kc n", p=128)
            wuv = wu_d[li].rearrange("(kc p) n -> p kc n", p=128)
            wdv = wd_d[li].rearrange("(s p) n -> p s n", p=128)

            def slot(s):
                base = s * 12288
                g = BW[:, base:base + 4096].rearrange("p (k n) -> p k n", k=16)
                u = BW[:, base + 4096:base + 8192].rearrange("p (k n) -> p k n", k=16)
                d = BW[:, base + 8192:base + 12288].rearrange("p (s n) -> p s n", s=2)
                return g, u, d

            def load(j):
                g, u, d = slot(j % 2)
                sem = f"f{j % 2}"
                dma("pool", g, wgv[:, :, j * CFF:(j + 1) * CFF], sem)
                dma("pool", u, wuv[:, :, j * CFF:(j + 1) * CFF], sem)
                dma("pool", d, wdv[:, 2 * j:2 * j + 2, :], sem)

            steps = [(j, ti) for j in range(NFF) for ti in range(len(tiles))]

            def gu(k, part):
                j, ti = steps[k]
                t0, n = tiles[ti]
                g, u, d = slot(j % 2)
                ab = k % 2
                s, which = part // 2, part % 2
                bG, bU = 2 * s, 2 * s + 1
                if which == 0:
                    for kc in range(16):
                        mm(ps[bG][:, 0:n], g[:, kc, s * 128:(s + 1) * 128], A[:, kc, t0:t0 + n], kc == 0, kc == 15)
                else:
                    for kc in range(16):
                        mm(ps[bU][:, 0:n], u[:, kc, s * 128:(s + 1) * 128], A[:, kc, t0:t0 + n], kc == 0, kc == 15)
                    sg = S32[:, 1024 + s * 512: 1024 + s * 512 + n]
                    act(sg, ps[bG][:, 0:n], AF.Silu)
                    av = S16[:, 4096 + ab * 1024 + s * 512: 4096 + ab * 1024 + s * 512 + n]
                    tt("dve", av, ps[bU][:, 0:n], sg, ALU.mult)

            def down(k, part):
                j, ti = steps[k]
                t0, n = tiles[ti]
                g, u, d = slot(j % 2)
                ab = k % 2
                for m in range(4 * part, 4 * part + 4):
                    bD = nb(4, 8)
                    for s in range(2):
                        av = S16[:, 4096 + ab * 1024 + s * 512: 4096 + ab * 1024 + s * 512 + n]
                        mm(ps[bD][:, 0:n], d[:, s, m * 128:(m + 1) * 128], av, s == 0, s == 1)
                    tt("dve", X[:, m, t0:t0 + n], ps[bD][:, 0:n], X[:, m, t0:t0 + n], ALU.add)

            load(0)
            load(1)
            nt = len(tiles)
            for part in range(4):
                gu(0, part)
            for k in range(len(steps)):
                for part in range(4):
                    if k + 1 < len(steps):
                        gu(k + 1, part)
                    down(k, part)
                j, ti = steps[k]
                if ti == nt - 1 and j + 2 < NFF:
                    load(j + 2)

        def ple(li, tiles):
            rmsnorm(64 * li + C_NP, tiles)
            ptb = S16[:, 4096:4096 + 2 * T].rearrange("p (c t) -> p c t", c=2)
            dma("pool", ptb, pT[li].rearrange("(c p) t -> p c t", p=128), "pt")
            wpp = BW[:, 24576:28672].rearrange("p (c n) -> p c n", c=2)
            dma("pool", wpp, wpp_d[li].rearrange("(c p) n -> p c n", p=128), "wpp")
            wv = wpg_d[li].rearrange("(kc p) n -> p kc n", p=128)

            def slot(m):
                q = m % 4
                return BW[:, 16384 + q * 2048:16384 + (q + 1) * 2048].rearrange("p (k n) -> p k n", k=16)

            def load(m):
                dma("pool", slot(m), wv[:, :, m * 128:(m + 1) * 128], f"g{m % 4}")

            for m in range(4):
                load(m)
            for m in range(NCH):
                w = slot(m)
                for (t0, n) in tiles:
                    bG = nb()
                    for kc in range(16):
                        mm(ps[bG][:, 0:n], w[:, kc, :], A[:, kc, t0:t0 + n], kc == 0, kc == 15)
                    bP = nb()
                    for c in range(2):
                        mm(ps[bP][:, 0:n], wpp[:, c, m * 128:(m + 1) * 128], ptb[:, c, t0:t0 + n], c == 0, c == 1)
                    sg = S32[:, 1024:1024 + n]
                    act(sg, ps[bG][:, 0:n], AF.Sigmoid, bias=col(64 * li + C_PB + m))
                    tmp = S32[:, 1536:1536 + n]
                    tt("dve", tmp, ps[bP][:, 0:n], sg, ALU.mult)
                    tt("pool", X[:, m, t0:t0 + n], X[:, m, t0:t0 + n], tmp, ALU.add)
                if m + 4 < NCH:
                    load(m + 4)

        def conv_mixer(li, j, tiles_in, tiles_out):
            cb = C_CONV + 96 * j
            rmsnorm(64 * li + C_NM, tiles_in)
            wdw = S32[:, 2048:2048 + 496]
            dma("sp", wdw, wdw_d[j], "wdw")
            wiv = conv_w_in[j].rearrange("(kc p) n -> p kc n", p=128)
            Bv = BW[:, 0:NCH * T].rearrange("p (c t) -> p c t", c=NCH)
            ub = [S16[:, 0:T], S16[:, T:2 * T]]
            diag = S16[:, 2 * T:2 * T + 31 * 128].rearrange("p (k n) -> p k n", k=31)
            tin0 = tiles_in[0][0]
            tout0 = tiles_out[0][0]

            def slot(m):
                base = 20480 + (m % 2) * 4096
                a = BW[:, base:base + 2048].rearrange("p (k n) -> p k n", k=16)
                g = BW[:, base + 2048:base + 4096].rearrange("p (k n) -> p k n", k=16)
                return a, g

            def load(m):
                a, g = slot(m)
                dma("pool", a, wiv[:, :, m * 128:(m + 1) * 128], f"c{m % 2}")
                dma("pool", g, wiv[:, :, D + m * 128:D + (m + 1) * 128], f"c{m % 2}")

            def inproj(m):
                a, g = slot(m)
                u = ub[m % 2]
                for (t0, n) in tiles_in:
                    bA = nb()
                    for kc in range(16):
                        mm(ps[bA][:, 0:n], a[:, kc, :], A[:, kc, t0:t0 + n], kc == 0, kc == 15)
                    bG = nb()
                    for kc in range(16):
                        mm(ps[bG][:, 0:n], g[:, kc, :], A[:, kc, t0:t0 + n], kc == 0, kc == 15)
                    sgb = (t0 // 512) % 2
                    sg = S32[:, 1024 + sgb * 512:1024 + sgb * 512 + n]
                    act(sg, ps[bG][:, 0:n], AF.Sigmoid, bias=col(cb + 16 + m))
                    if t0 < HALO:
                        mlen = min(HALO, t0 + n) - t0
                        tt("pool", sg[:, 0:mlen], sg[:, 0:mlen], tokmask[:, t0:t0 + mlen], ALU.mult)
                    stt(u[:, t0:t0 + n], ps[bA][:, 0:n], col(cb + m), sg, ALU.add, ALU.mult)

            def dwconv(m):
                u = ub[m % 2]
                for k in range(31):
                    amul(diag[:, k, :], ident[:], wdw[:, k * 16 + m:k * 16 + m + 1])
                for (t0, n) in tiles_out:
                    o0 = max(t0, tin0 + 32)
                    nn = t0 + n - o0
                    bC = nb()
                    for k in range(31):
                        mm(ps[bC][:, 0:nn], diag[:, k, :], u[:, o0 - 30 + k:o0 - 30 + k + nn], k == 0, k == 30)
                    act(Bv[:, m, o0:o0 + nn], ps[bC][:, 0:nn], AF.Identity, bias=col(cb + 32 + m))

            load(0)
            load(1)
            if tout0 < tin0 + 32:
                memset("pool", Bv[:, :, tout0:tin0 + 32], 0.0)
            for m in range(NCH + 1):
                if m < NCH:
                    inproj(m)
                if m >= 1:
                    dwconv(m - 1)
                    if m + 1 < NCH:
                        load(m + 1)
            for (t0, n) in tiles_out:
                b1 = nb()
                b2 = nb()
                for c4 in range(4):
                    sb_i = sq_ctr[0] % 2
                    sq_ctr[0] += 1
                    for cc in range(4):
                        c = c4 * 4 + cc
                        sq = S16[:, sb_i * 2048 + cc * 512: sb_i * 2048 + cc * 512 + n]
                        act(sq, Bv[:, c, t0:t0 + n], AF.Square)
                    for cc in range(4):
                        c = c4 * 4 + cc
                        sq = S16[:, sb_i * 2048 + cc * 512: sb_i * 2048 + cc * 512 + n]
                        mm(ps[b1][:, 0:n], ones[:], Bv[:, c, t0:t0 + n], c == 0, c == 15)
                        mm(ps[b2][:, 0:n], ones[:], sq, c == 0, c == 15)
                mean = S32[:, 0:n]
                rstd = S32[:, 512:512 + n]
                msq = S32[:, 1024:1024 + n]
                act(mean, ps[b1][:, 0:n], AF.Identity, scale=1.0 / D)
                tt("dve", msq, mean, mean, ALU.mult)
                stt(rstd, ps[b2][:, 0:n], 1.0 / D, msq, ALU.mult, ALU.subtract)
                ts(rstd, rstd, 0.0, None, ALU.max)
                act(rstd, rstd, AF.Sqrt, bias=col(C_EPS))
                recip(rstd, rstd)
                for c in range(NCH):
                    t1 = S32[:, 2560:2560 + n] if c % 2 else S32[:, 1536:1536 + n]
                    tt("dve", t1, Bv[:, c, t0:t0 + n], mean, ALU.subtract)
                    tt("dve", t1, t1, rstd, ALU.mult)
                    act(A[:, c, t0:t0 + n], t1, AF.Silu, bias=col(cb + 64 + c), scale=col(cb + 48 + c))
            wov = conv_w_out[j].rearrange("(kc p) n -> p kc n", p=128)

            def oslot(m2):
                base = 20480 + (m2 % 2) * 4096
                return BW[:, base:base + 4096].rearrange("p (k n) -> p k n", k=16)

            def oload(m2):
                dma("pool", oslot(m2), wov[:, :, m2 * 256:(m2 + 1) * 256], f"c{m2 % 2}")

            oload(0)
            oload(1)
            for m2 in range(8):
                w = oslot(m2)
                for mi in range(2):
                    m = m2 * 2 + mi
                    for (t0, n) in tiles_out:
                        bO = nb()
                        for kc in range(16):
                            mm(ps[bO][:, 0:n], w[:, kc, mi * 128:(mi + 1) * 128], A[:, kc, t0:t0 + n], kc == 0, kc == 15)
                        stt(X[:, m, t0:t0 + n], ps[bO][:, 0:n], col(cb + 80 + m), X[:, m, t0:t0 + n], ALU.add, ALU.add)
                if m2 + 2 < 8:
                    oload(m2 + 2)

        def pool_mixer(li, tiles):
            rmsnorm(64 * li + C_NM, tiles, mask_halo=True)
            Bv = BW[:, 0:NCH * T].rearrange("p (c t) -> p c t", c=NCH)
            wp = BW[:, 20480:28672].rearrange("p (g k n) -> p g k n", g=4, k=4)
            for g in range(4):
                dma("pool", wp[:, g], pool_w[g].rearrange("(k p) n -> p k n", p=128), "pw")
            memset("pool", Bv[:, :, 0:16], 0.0)
            for c in range(NCH):
                g = c // 4
                L = g + 1
                w = 2 ** L
                base = (c % 2) * 1536
                for (o0, o1) in ((16, 656), (656, T)):
                    i0 = o0 - 16
                    ln = o1 - i0
                    tA = S32[:, base:base + ln]
                    tB = S32[:, base + 672:base + 672 + ln]
                    src = A[:, c, i0:o1]
                    cur = src
                    bufs = [tA, tB]
                    for l in range(1, L + 1):
                        sh = 2 ** (l - 1)
                        lo = 2 ** l - 1
                        dst = bufs[(l - 1) % 2]
                        tt("pool", dst[:, lo:ln], cur[:, lo:ln], cur[:, lo - sh:ln - sh], ALU.add)
                        cur = dst
                    stt(Bv[:, c, o0:o1], cur[:, 16:ln], 1.0 / w, A[:, c, o0:o1], ALU.mult, ALU.subtract)
                    if o0 <= HALO < o1:
                        r0 = HALO - i0
                        t16 = S32[:, base + 1344:base + 1360]
                        tt("dve", t16, cur[:, r0:r0 + 16], invc[:, g * 16:(g + 1) * 16], ALU.mult)
                        tt("dve", Bv[:, c, HALO:HALO + 16], t16, A[:, c, HALO:HALO + 16], ALU.subtract)
            for g in range(4):
                for mo in range(4):
                    m = g * 4 + mo
                    for (t0, n) in tiles:
                        bO = nb()
                        for kc in range(4):
                            mm(ps[bO][:, 0:n], wp[:, g, kc, mo * 128:(mo + 1) * 128], Bv[:, g * 4 + kc, t0:t0 + n], kc == 0, kc == 3)
                        stt(X[:, m, t0:t0 + n], ps[bO][:, 0:n], col(C_PSC + m), X[:, m, t0:t0 + n], ALU.mult, ALU.add)

        def attn_mixer(li, tiles_q):
            rmsnorm(64 * li + C_NM, tiles_from(0))
            TQ = T - 128
            Qv = BW[:, 0:NCH * TQ].rearrange("p (c t) -> p c t", c=NCH)
            Kv = BW[:, 18432:18432 + 4 * T].rearrange("p (c t) -> p c t", c=4)
            Vv = BW[:, 23552:23552 + 10 * 512].rearrange("p (b n) -> p b n", b=10)
            Ov = A[:, :, 0:TQ]

            def ws(i):
                b = 2048 + (i % 2) * 2048
                return S16[:, b:b + 2048].rearrange("p (k n) -> p k n", k=16)

            widx = [0]

            def load_w(src_ap):
                i = widx[0]
                widx[0] += 1
                w = ws(i)
                dma("pool", w, src_ap, f"a{i % 2}")
                return w

            def qknorm(psb, n, gcol, dst):
                sq = S16[:, 0:n] if (sq_ctr[0] % 2 == 0) else S16[:, 512:512 + n]
                sq_ctr[0] += 1
                act(sq, ps[psb][:, 0:n], AF.Square)
                b2 = nb()
                mm(ps[b2][:, 0:n], bd[:], sq, True, True)
                rs = S32[:, 0:n]
                act(rs, ps[b2][:, 0:n], AF.Sqrt, bias=col(C_EPS), scale=1.0 / 64)
                recip(rs, rs)
                stt(dst, ps[psb][:, 0:n], col(gcol), rs, ALU.mult, ALU.mult)

            wkv = wk_d.rearrange("(kc p) n -> p kc n", p=128)
            wvv = wv_d.rearrange("(kc p) n -> p kc n", p=128)
            wqv = wq_d.rearrange("(kc p) n -> p kc n", p=128)
            wov = wo_d.rearrange("(kc p) n -> p kc n", p=128)
            for Pp in range(4):
                w = load_w(wkv[:, :, Pp * 128:(Pp + 1) * 128])
                for (t0, n) in tiles_from(0):
                    b = nb()
                    for kc in range(16):
                        mm(ps[b][:, 0:n], w[:, kc, :], A[:, kc, t0:t0 + n], kc == 0, kc == 15)
                    qknorm(b, n, C_GK, Kv[:, Pp, t0:t0 + n])
            for Pp in range(4):
                w = load_w(wvv[:, :, Pp * 128:(Pp + 1) * 128])
                for bl in range(10):
                    b = nb()
                    for kc in range(16):
                        mm(ps[b][:, 0:128], A[:, kc, bl * 128:(bl + 1) * 128], w[:, kc, :], kc == 0, kc == 15)
                    act(Vv[:, bl, Pp * 128:(Pp + 1) * 128], ps[b][:, 0:128], AF.Identity)
            for qc in range(NCH):
                w = load_w(wqv[:, :, qc * 128:(qc + 1) * 128])
                for (t0, n) in tiles_q:
                    b = nb()
                    for kc in range(16):
                        mm(ps[b][:, 0:n], w[:, kc, :], A[:, kc, t0:t0 + n], kc == 0, kc == 15)
                    qknorm(b, n, C_GQ8, Qv[:, qc, t0 - 128:t0 - 128 + n])
            maskb = S32[:, 1024:1280].rearrange("p (k q) -> p k q", k=2)
            dma("sp", maskb, maskT_d, "bias")
            biasm = S32[:, 0:1024].rearrange("p (g k q) -> p g k q", g=4, k=2)
            for kv in range(8):
                Pp, half = kv // 2, kv % 2
                pr = slice(half * 64, half * 64 + 64)
                dma("sp", biasm, biasT_d[:, kv * 4:(kv + 1) * 4], "bias")
                for g in range(4):
                    tt("pool", biasm[:, g], biasm[:, g], maskb, ALU.add)
                for b in range(1, 10):
                    eb = (b % 2) * 1024
                    for kc in range(2):
                        kb = b - 1 + kc
                        bS = nb()
                        for g in range(4):
                            mm(ps[bS][:, g * 128:(g + 1) * 128], Kv[pr, Pp, kb * 128:(kb + 1) * 128],
                               Qv[pr, Pp * 4 + g, (b - 1) * 128:b * 128], True, True)
                        tbuf = S32[:, 1536 + kc * 512:1536 + (kc + 1) * 512]
                        tt("dve", tbuf.rearrange("p (g q) -> p g q", g=4), ps[bS][:, 0:512].rearrange("p (g q) -> p g q", g=4),
                           biasm[:, :, kc, :], ALU.add)
                        e = S16[:, eb + kc * 512:eb + (kc + 1) * 512]
                        act(e, tbuf, AF.Exp, bias=kbias[:, kb:kb + 1])
                    bO = nb()
                    bDn = nb()
                    for kc in range(2):
                        kb = b - 1 + kc
                        e = S16[:, eb + kc * 512:eb + (kc + 1) * 512]
                        mm(ps[bO][:, 0:512], Vv[:, kb, Pp * 128:(Pp + 1) * 128], e, kc == 0, kc == 1)
                    for kc in range(2):
                        e = S16[:, eb + kc * 512:eb + (kc + 1) * 512]
                        mm(ps[bDn][:, 0:512], ones[:], e, kc == 0, kc == 1)
                    rden = S32[:, 2560:3072]
                    for g in range(4):
                        ts(rden[pr, g * 128:(g + 1) * 128], ps[bDn][pr, g * 128:(g + 1) * 128], col(C_SINK + kv * 4 + g)[pr], None, ALU.add)
                    recip(rden[pr], rden[pr])
                    tt("dve", Ov[pr, Pp * 4:Pp * 4 + 4, (b - 1) * 128:b * 128],
                       ps[bO][pr, 0:512].rearrange("p (g q) -> p g q", g=4), rden[pr].rearrange("p (g q) -> p g q", g=4), ALU.mult)
            for m in range(NCH):
                w = load_w(wov[:, :, m * 128:(m + 1) * 128])
                for (t0, n) in tiles_q:
                    b = nb()
                    for kc in range(16):
                        mm(ps[b][:, 0:n], w[:, kc, :], Ov[:, kc, t0 - 128:t0 - 128 + n], kc == 0, kc == 15)
                    tt("dve", X[:, m, t0:t0 + n], ps[b][:, 0:n], X[:, m, t0:t0 + n], ALU.add)

        for li in range(nlayers):
            kind, j = li % 3, li // 3
            if li < 2:
                tl = tiles_from(0)
            elif li == 2:
                tl = tiles_from(128)
            else:
                tl = tiles_from(256)
            if kind == 0:
                if li == 0:
                    conv_mixer(li, j, tiles_from(0), tiles_from(0))
                else:
                    conv_mixer(li, j, tiles_from(128), tiles_from(256))
            elif kind == 1:
                pool_mixer(li, tl)
            else:
                attn_mixer(li, tl)
            ffn(li, tl)
            ple(li, tl)

        yv = yT.rearrange("(c p) t -> p c t", p=128)
        if dump_x:
            dma("sp", yv, X[:], "o")
        else:
            dma("sp", yv, X[:, :, HALO:T], "o")
        P.finalize()
        stats = P.emit(block, sems, dsems, final_waits={"sp": ["o"]})
    return nc, stats


def _t5_bucket(rel):
    nb_ = 16
    n = -rel
    ret = np.where(n < 0, nb_, 0)
    n = np.abs(n)
    max_exact = nb_ // 2
    nf = np.maximum(n, 1).astype(np.float32)
    large = max_exact + (np.log(nf / max_exact) / math.log(128 / max_exact) * (nb_ - max_exact)).astype(np.int32)
    large = np.minimum(large, nb_ - 1)
    return ret + np.where(n < max_exact, n, large)


def _colpack(v):
    v = np.asarray(v, np.float32)
    return np.ascontiguousarray(v.reshape(-1, 128).T)


def prepare_inputs(inp):
    f32 = np.float32
    x = np.asarray(inp["x"], f32)[0]
    p = np.asarray(inp["p"], f32)[:, 0]
    cols = np.zeros((128, NCOL), f32)
    for li in range(DEPTH):
        cols[:, 64 * li + C_NM:64 * li + C_NM + 16] = _colpack(inp["norm_mix"][li])
        cols[:, 64 * li + C_NF:64 * li + C_NF + 16] = _colpack(inp["norm_ffn"][li])
        cols[:, 64 * li + C_NP:64 * li + C_NP + 16] = _colpack(inp["norm_ple"][li])
        cols[:, 64 * li + C_PB:64 * li + C_PB + 16] = _colpack(inp["ple_b_gate"][li])
    for j in range(2):
        cb = C_CONV + 96 * j
        cols[:, cb:cb + 32] = _colpack(inp["conv_b_in"][j])
        cols[:, cb + 32:cb + 48] = _colpack(inp["conv_b_dw"][j])
        cols[:, cb + 48:cb + 64] = _colpack(inp["conv_ln_g"][j])
        cols[:, cb + 64:cb + 80] = _colpack(inp["conv_ln_b"][j])
        cols[:, cb + 80:cb + 96] = _colpack(inp["conv_b_out"][j])
    cols[:, C_PSC:C_PSC + 16] = _colpack(inp["pool_scale"][0])
    cols[:, C_GQ] = np.tile(np.asarray(inp["attn_q_norm"][0], f32), 2)
    cols[:, C_GK] = np.tile(np.asarray(inp["attn_k_norm"][0], f32), 2)
    cols[:, C_EPS] = EPS
    cols[:, C_SINK:C_SINK + 32] = np.asarray(inp["attn_sinks"][0], f32)[None, :]
    wdw = np.zeros((2, 128, 496), f32)
    for j in range(2):
        w = np.asarray(inp["conv_w_dw"][j], f32)
        wdw[j] = w.reshape(31, 16, 128).transpose(2, 0, 1).reshape(128, 496)
    wqkv = np.asarray(inp["attn_w_qkv"][0], f32)
    wq = wqkv[:, :2048]
    perm = np.zeros(2048, np.int64)
    for Pp in range(4):
        for g in range(4):
            for half in range(2):
                h = (2 * Pp + half) * 4 + g
                qc = Pp * 4 + g
                perm[qc * 128 + half * 64:qc * 128 + half * 64 + 64] = np.arange(h * 64, h * 64 + 64)
    wq_p = np.ascontiguousarray(wq[:, perm])
    wk = np.ascontiguousarray(wqkv[:, 2048:2560])
    wv = np.ascontiguousarray(wqkv[:, 2560:3072])
    wo_p = np.ascontiguousarray(np.asarray(inp["attn_w_o"][0], f32)[perm, :])
    rb = np.asarray(inp["rel_bias"], f32)
    i = np.arange(128)[None, :]
    biasT = np.zeros((128, 32, 2, 128), f32)
    maskT = np.zeros((128, 2, 128), f32)
    for kc in range(2):
        jj = np.arange(128)[:, None] + 128 * kc
        rel = jj - 128 - i
        bk = _t5_bucket(rel)
        biasT[:, :, kc, :] = rb[bk].transpose(0, 2, 1)
        qc_ = i // 64
        kc_ = np.floor_divide(jj - 128, 64)
        ok = (kc_ <= qc_) & (kc_ >= qc_ - 2)
        maskT[:, kc, :] = np.where(ok, 0.0, NEG)
    ident = np.eye(128, dtype=f32)
    shared = {
        "cols": cols, "ident": ident, "wdw": wdw,
        "conv_w_in": np.asarray(inp["conv_w_in"], f32), "conv_w_out": np.asarray(inp["conv_w_out"], f32),
        "pool_w": np.asarray(inp["pool_w"], f32)[0],
        "wq": wq_p, "wk": wk, "wv": wv, "wo": wo_p, "biasT": biasT, "maskT": maskT,
        "ffn_w_gate": np.asarray(inp["ffn_w_gate"], f32), "ffn_w_up": np.asarray(inp["ffn_w_up"], f32),
        "ffn_w_down": np.asarray(inp["ffn_w_down"], f32),
        "ple_w_proj": np.asarray(inp["ple_w_proj"], f32), "ple_w_gate": np.asarray(inp["ple_w_gate"], f32),
    }
    in_maps = []
    for c in range(NCORE):
        T0 = c * TOK
        lo = T0 - HALO
        xT = np.zeros((D, T), f32)
        pTc = np.zeros((DEPTH, PLE, T), f32)
        s = max(lo, 0)
        xT[:, s - lo:] = x[s:T0 + TOK].T
        pTc[:, :, s - lo:] = p[:, s:T0 + TOK].transpose(0, 2, 1)
        gpos = lo + np.arange(HALO)
        tokmask = np.broadcast_to((gpos >= 0).astype(f32)[None, :], (128, HALO)).copy()
        kb = np.zeros((128, 16), f32)
        for b in range(10):
            if lo + 128 * b < 0:
                kb[:, b] = NEG
        invc = np.zeros((128, 64), f32)
        for g in range(4):
            w = 2 ** (g + 1)
            gt = T0 + np.arange(16)
            cnt = np.minimum(gt + 1, w).astype(f32)
            invc[:, g * 16:(g + 1) * 16] = (1.0 / cnt)[None, :]
        m = dict(shared)
        m.update({"xT": xT, "pT": pTc, "tokmask": tokmask, "kbias": kb, "invc": invc})
        in_maps.append(m)
    return in_maps


_CACHE = {}


def kernel(**inputs):
    if "nc" not in _CACHE:
        _CACHE["nc"] = build_program()[0]
    nc = _CACHE["nc"]
    in_maps = prepare_inputs(inputs)
    res = run_bass_kernel_spmd(nc, in_maps, core_ids=list(range(NCORE)))
    out = np.zeros((1, SEQ, D), np.float32)
    for c in range(NCORE):
        out[0, c * TOK:(c + 1) * TOK, :] = res.results[c]["yT"].T
    return out
```
